# Optimizing a Trainium2 kernel written in Bass

```python
import math
import jax
import jax.numpy as jnp
from jax import lax
import numpy as np

D_MODEL = 1024
BATCH = 32
SEQ = 2048
DEPTH = 2
DEC_BATCH = 32
DEC_SEQ = 16
PAST_LEN = 4096

CHUNK = 64
EPS = 1e-6
PLE_DIM = 256
NEG_INF = -1e30

WINDOW = 128
N_WIN_CHUNKS = WINDOW // CHUNK
ATTN_HEADS = 8
ATTN_KV_HEADS = 2
ATTN_GROUP = ATTN_HEADS // ATTN_KV_HEADS
HEAD_DIM = 64
ROPE_THETA = 10000.0
Q_DIM = ATTN_HEADS * HEAD_DIM
KV_DIM = ATTN_KV_HEADS * HEAD_DIM
WIN_ROWS = min(WINDOW, PAST_LEN)

SSD_HEADS = 16
SSD_HEAD_DIM = 64
SSD_INNER = SSD_HEADS * SSD_HEAD_DIM
SSD_GROUPS = 2
SSD_HPG = SSD_HEADS // SSD_GROUPS
SSD_STATE = 128
SSD_CONV = 4
SSD_BC = SSD_GROUPS * SSD_STATE
SSD_CONV_DIM = SSD_INNER + 2 * SSD_BC

SC_CONV = 3

HYB_IN = Q_DIM + 2 * KV_DIM + SSD_INNER + SSD_CONV_DIM + SSD_HEADS
HYB_OUT = Q_DIM + SSD_INNER
HYB_SPLITS = (Q_DIM, Q_DIM + KV_DIM, Q_DIM + 2 * KV_DIM, Q_DIM + 2 * KV_DIM + SSD_INNER, Q_DIM + 2 * KV_DIM + SSD_INNER + SSD_CONV_DIM)

FF_RAW = -(-8 * D_MODEL // 3)
D_FF = -(-FF_RAW // 256) * 256

kernel_name = 'hybrid_stream_encoder_step'


def rmsnorm(x, w):
    x32 = x.astype(jnp.float32)
    y = x32 * lax.rsqrt(jnp.mean(x32 * x32, axis=-1, keepdims=True) + EPS)
    return (y * w.astype(jnp.float32)).astype(x.dtype)


def rope(x, pos):
    half = HEAD_DIM // 2
    inv = ROPE_THETA ** (-jnp.arange(half, dtype=jnp.float32) / half)
    ang = pos.astype(jnp.float32)[:, None] * inv[None, :]
    cos = jnp.cos(ang)[:, None, :]
    sin = jnp.sin(ang)[:, None, :]
    x32 = x.astype(jnp.float32)
    x1, x2 = x32[..., :half], x32[..., half:]
    return jnp.concatenate([x1 * cos - x2 * sin, x2 * cos + x1 * sin], axis=-1).astype(x.dtype)


def causal_dwconv(u, w, prev):
    K = w.shape[0]
    L = u.shape[1]
    full = jnp.concatenate([prev.astype(u.dtype), u], axis=1)
    out = full[:, 0:L] * w[0]
    for j in range(1, K):
        out = out + full[:, j:j + L] * w[j]
    return out, full[:, L:]


def sink_attention(q, k, v, sinks, valid):
    s = jnp.einsum('bnqkgd,bnskd->bnkgqs', q.astype(jnp.float32), k.astype(jnp.float32)) * (HEAD_DIM ** -0.5)
    if valid is not None:
        s = jnp.where(valid[None, :, None, None, None, :], s, NEG_INF)
    sink = sinks.astype(jnp.float32).reshape(ATTN_KV_HEADS, ATTN_GROUP)[None, None, :, :, None]
    m = jnp.maximum(s.max(axis=-1), sink)
    p = jnp.exp(s - m[..., None])
    denom = p.sum(axis=-1) + jnp.exp(sink - m)
    o = jnp.einsum('bnkgqs,bnskd->bnqkgd', p, v.astype(jnp.float32))
    o = o / jnp.moveaxis(denom, -1, 2)[..., None]
    return o.astype(q.dtype)


def window_attention_prompt(q, k, v, sinks):
    b, L = q.shape[:2]
    nc = L // CHUNK
    qb = q.reshape(b, nc, CHUNK, ATTN_KV_HEADS, ATTN_GROUP, HEAD_DIM)
    pad = ((0, 0), (N_WIN_CHUNKS * CHUNK, 0), (0, 0), (0, 0))
    kp = jnp.pad(k, pad).reshape(b, nc + N_WIN_CHUNKS, CHUNK, ATTN_KV_HEADS, HEAD_DIM)
    vp = jnp.pad(v, pad).reshape(b, nc + N_WIN_CHUNKS, CHUNK, ATTN_KV_HEADS, HEAD_DIM)
    kb = jnp.concatenate([kp[:, j:j + nc] for j in range(N_WIN_CHUNKS + 1)], axis=2)
    vb = jnp.concatenate([vp[:, j:j + nc] for j in range(N_WIN_CHUNKS + 1)], axis=2)
    key_chunk = jnp.arange(nc)[:, None] - N_WIN_CHUNKS + jnp.repeat(jnp.arange(N_WIN_CHUNKS + 1), CHUNK)[None, :]
    o = sink_attention(qb, kb, vb, sinks, key_chunk >= 0)
    return o.reshape(b, L, Q_DIM)


def window_attention_sample(q, k, v, k_cache, v_cache, sinks):
    b, L = q.shape[:2]
    qb = q.reshape(b, 1, L, ATTN_KV_HEADS, ATTN_GROUP, HEAD_DIM)
    kb = jnp.concatenate([k_cache.astype(k.dtype), k], axis=1)[:, None]
    vb = jnp.concatenate([v_cache.astype(v.dtype), v], axis=1)[:, None]
    o = sink_attention(qb, kb, vb, sinks, None)
    return o.reshape(b, L, Q_DIM)


def ssd_scan(x, dt, a, bm, cm, h0):
    b, L = x.shape[:2]
    T = min(CHUNK, L)
    nc = L // T

    def blk(t):
        return t.reshape((b, nc, T) + t.shape[2:])

    x, dt, bm, cm = blk(x), blk(dt), blk(bm), blk(cm)
    cum = jnp.cumsum(dt * a, axis=2)
    seg = cum[:, :, :, None] - cum[:, :, None, :]
    causal = jnp.tril(jnp.ones((T, T), dtype=bool))[:, :, None, None]
    decay = jnp.where(causal, jnp.exp(jnp.where(causal, seg, 0.0)), 0.0)
    cb = jnp.einsum('bctgn,bcsgn->bctsg', cm, bm)
    w_ts = cb[..., None] * decay * dt[:, :, None]
    y_intra = jnp.einsum('bctsgh,bcsghp->bctghp', w_ts, x)
    last = cum[:, :, -1]
    w_state = jnp.exp(last[:, :, None] - cum) * dt
    s_chunk = jnp.einsum('bcsgh,bcsghp,bcsgn->bcghpn', w_state, x, bm)

    def step(h, inp):
        dec, s = inp
        return jnp.exp(dec)[..., None, None] * h + s, h

    h_final, h_prev = lax.scan(step, h0, (jnp.moveaxis(last, 1, 0), jnp.moveaxis(s_chunk, 1, 0)))
    h_prev = jnp.moveaxis(h_prev, 0, 1)
    y_inter = jnp.einsum('bctgn,bcghpn->bctghp', cm, h_prev) * jnp.exp(cum)[..., None]
    y = (y_intra + y_inter).reshape(b, L, SSD_GROUPS, SSD_HPG, SSD_HEAD_DIM)
    return y, h_final


def ssd_mixer(z, xbc, dt_raw, h0, conv0, conv_w, conv_b, dt_bias, a_log, d_skip, norm_w):
    b, L, _ = z.shape
    xbc, conv_state = causal_dwconv(xbc, conv_w, conv0)
    xbc = jax.nn.silu(xbc + conv_b).astype(jnp.float32)
    xs = xbc[..., :SSD_INNER].reshape(b, L, SSD_GROUPS, SSD_HPG, SSD_HEAD_DIM)
    bm = xbc[..., SSD_INNER:SSD_INNER + SSD_BC].reshape(b, L, SSD_GROUPS, SSD_STATE)
    cm = xbc[..., SSD_INNER + SSD_BC:].reshape(b, L, SSD_GROUPS, SSD_STATE)
    dt = jax.nn.softplus(dt_raw.astype(jnp.float32) + dt_bias.astype(jnp.float32)).reshape(b, L, SSD_GROUPS, SSD_HPG)
    a = -jnp.exp(a_log.astype(jnp.float32)).reshape(SSD_GROUPS, SSD_HPG)
    h0 = h0.astype(jnp.float32).reshape(b, SSD_GROUPS, SSD_HPG, SSD_HEAD_DIM, SSD_STATE)
    y, h = ssd_scan(xs, dt, a, bm, cm, h0)
    y = y + d_skip.astype(jnp.float32).reshape(SSD_GROUPS, SSD_HPG)[..., None] * xs
    gw = SSD_HPG * SSD_HEAD_DIM
    y = y.reshape(b, L, SSD_GROUPS, gw) * jax.nn.silu(z.astype(jnp.float32).reshape(b, L, SSD_GROUPS, gw))
    y = y * lax.rsqrt(jnp.mean(y * y, axis=-1, keepdims=True) + EPS) * norm_w.astype(jnp.float32).reshape(SSD_GROUPS, gw)
    return y.reshape(b, L, SSD_INNER).astype(z.dtype), conv_state, h.reshape(b, SSD_HEADS, SSD_HEAD_DIM, SSD_STATE)


def hybrid_mixer(xn, pos, win_k, win_v, ssm0, conv0, w_in, w_out, sinks, conv_w, conv_b, dt_bias, a_log, d_skip, norm_w):
    b, L, _ = xn.shape
    q, k, v, z, xbc, dt_raw = jnp.split(xn @ w_in, HYB_SPLITS, axis=-1)
    q = rope(q.reshape(b, L, ATTN_HEADS, HEAD_DIM), pos)
    k = rope(k.reshape(b, L, ATTN_KV_HEADS, HEAD_DIM), pos)
    v = v.reshape(b, L, ATTN_KV_HEADS, HEAD_DIM)
    if win_k is None:
        attn = window_attention_prompt(q, k, v, sinks)
        k_state, v_state = k[:, -WIN_ROWS:], v[:, -WIN_ROWS:]
    else:
        attn = window_attention_sample(q, k, v, win_k, win_v, sinks)
        k_state, v_state = k, v
    ssd, conv_state, ssm_state = ssd_mixer(z, xbc, dt_raw, ssm0, conv0, conv_w, conv_b, dt_bias, a_log, d_skip, norm_w)
    out = jnp.concatenate([attn, ssd], axis=-1) @ w_out
    return out, k_state, v_state, ssm_state.astype(xn.dtype), conv_state


def short_conv_mixer(xn, prev, w_in, conv_w, w_out):
    bg, cg, h = jnp.split(xn @ w_in, 3, axis=-1)
    u, state = causal_dwconv(cg * h, conv_w, prev)
    return (bg * u) @ w_out, state


def swiglu(xn, w_gate, w_up, w_down):
    return (jax.nn.silu(xn @ w_gate) * (xn @ w_up)) @ w_down


def trunk(x, p, pos0, win_k, win_v, ssm0, ssd_conv0, sc_conv0,
          norm_mix, norm_ffn, norm_ple, final_norm,
          hyb_w_in, hyb_w_out, attn_sinks, ssd_conv_w, ssd_conv_b, ssd_dt_bias, ssd_a_log, ssd_d, ssd_norm_w,
          sc_w_in, sc_conv_w, sc_w_out,
          ffn_w_gate, ffn_w_up, ffn_w_down, ple_w_proj, ple_w_gate):
    pos = pos0 + jnp.arange(x.shape[1], dtype=jnp.int32)
    for i in range(DEPTH):
        xn = rmsnorm(x, norm_mix[i])
        if i % 2 == 0:
            mix, k_state, v_state, ssm_state, ssd_conv_state = hybrid_mixer(
                xn, pos, win_k, win_v, ssm0, ssd_conv0, hyb_w_in, hyb_w_out, attn_sinks,
                ssd_conv_w, ssd_conv_b, ssd_dt_bias, ssd_a_log, ssd_d, ssd_norm_w)
        else:
            mix, sc_state = short_conv_mixer(xn, sc_conv0, sc_w_in, sc_conv_w, sc_w_out)
        x = x + mix
        x = x + swiglu(rmsnorm(x, norm_ffn[i]), ffn_w_gate[i], ffn_w_up[i], ffn_w_down[i])
        gate = jax.nn.sigmoid(rmsnorm(x, norm_ple[i]) @ ple_w_gate[i])
        x = x + gate * (p[i].astype(x.dtype) @ ple_w_proj[i])
    return rmsnorm(x, final_norm), k_state, v_state, ssm_state, ssd_conv_state, sc_state


def setup_inputs(seed: int = 0) -> dict:
    key = jax.random.key(seed)
    ks = jax.random.split(key, 30)
    f32 = jnp.float32

    def nrm(k, shape, scale):
        return jax.random.normal(k, shape, f32) * scale

    dt0 = jnp.exp(jax.random.uniform(ks[18], (SSD_HEADS,), f32, math.log(1e-3), math.log(1e-1)))
    return {
        'x_prompt': nrm(ks[0], (BATCH, SEQ, D_MODEL), 1.0),
        'x_sample': nrm(ks[1], (DEC_BATCH, DEC_SEQ, D_MODEL), 1.0),
        'p_prompt': nrm(ks[2], (DEPTH, BATCH, SEQ, PLE_DIM), 1.0),
        'p_sample': nrm(ks[3], (DEPTH, DEC_BATCH, DEC_SEQ, PLE_DIM), 1.0),
        'cache_win_k': nrm(ks[4], (DEC_BATCH, WIN_ROWS, ATTN_KV_HEADS, HEAD_DIM), 1.0),
        'cache_win_v': nrm(ks[5], (DEC_BATCH, WIN_ROWS, ATTN_KV_HEADS, HEAD_DIM), 1.0),
        'state_ssm': nrm(ks[6], (DEC_BATCH, SSD_HEADS, SSD_HEAD_DIM, SSD_STATE), 0.5),
        'state_ssd_conv': nrm(ks[7], (DEC_BATCH, SSD_CONV - 1, SSD_CONV_DIM), 1.0),
        'state_short_conv': nrm(ks[8], (DEC_BATCH, SC_CONV - 1, D_MODEL), 1.0),
        'norm_mix': 1.0 + nrm(ks[9], (DEPTH, D_MODEL), 0.02),
        'norm_ffn': 1.0 + nrm(ks[10], (DEPTH, D_MODEL), 0.02),
        'norm_ple': 1.0 + nrm(ks[11], (DEPTH, D_MODEL), 0.02),
        'final_norm': 1.0 + nrm(ks[12], (D_MODEL,), 0.02),
        'hyb_w_in': nrm(ks[13], (D_MODEL, HYB_IN), D_MODEL ** -0.5),
        'hyb_w_out': nrm(ks[14], (HYB_OUT, D_MODEL), HYB_OUT ** -0.5),
        'attn_sinks': nrm(ks[15], (ATTN_HEADS,), 1.0),
        'ssd_conv_w': nrm(ks[16], (SSD_CONV, SSD_CONV_DIM), SSD_CONV ** -0.5),
        'ssd_conv_b': nrm(ks[17], (SSD_CONV_DIM,), 0.01),
        'ssd_dt_bias': dt0 + jnp.log(-jnp.expm1(-dt0)),
        'ssd_a_log': jnp.log(jax.random.uniform(ks[19], (SSD_HEADS,), f32, 1.0, 16.0)),
        'ssd_d': 1.0 + nrm(ks[20], (SSD_HEADS,), 0.02),
        'ssd_norm_w': 1.0 + nrm(ks[21], (SSD_INNER,), 0.02),
        'sc_w_in': nrm(ks[22], (D_MODEL, 3 * D_MODEL), D_MODEL ** -0.5),
        'sc_conv_w': nrm(ks[23], (SC_CONV, D_MODEL), SC_CONV ** -0.5),
        'sc_w_out': nrm(ks[24], (D_MODEL, D_MODEL), D_MODEL ** -0.5),
        'ffn_w_gate': nrm(ks[25], (DEPTH, D_MODEL, D_FF), D_MODEL ** -0.5),
        'ffn_w_up': nrm(ks[26], (DEPTH, D_MODEL, D_FF), D_MODEL ** -0.5),
        'ffn_w_down': nrm(ks[27], (DEPTH, D_FF, D_MODEL), D_FF ** -0.5),
        'ple_w_proj': nrm(ks[28], (DEPTH, PLE_DIM, D_MODEL), PLE_DIM ** -0.5),
        'ple_w_gate': nrm(ks[29], (DEPTH, D_MODEL, D_MODEL), D_MODEL ** -0.5),
    }


def reference(x_prompt, x_sample, p_prompt, p_sample, cache_win_k, cache_win_v, state_ssm, state_ssd_conv, state_short_conv,
              norm_mix, norm_ffn, norm_ple, final_norm,
              hyb_w_in, hyb_w_out, attn_sinks, ssd_conv_w, ssd_conv_b, ssd_dt_bias, ssd_a_log, ssd_d, ssd_norm_w,
              sc_w_in, sc_conv_w, sc_w_out,
              ffn_w_gate, ffn_w_up, ffn_w_down, ple_w_proj, ple_w_gate):
    weights = (norm_mix, norm_ffn, norm_ple, final_norm,
               hyb_w_in, hyb_w_out, attn_sinks, ssd_conv_w, ssd_conv_b, ssd_dt_bias, ssd_a_log, ssd_d, ssd_norm_w,
               sc_w_in, sc_conv_w, sc_w_out,
               ffn_w_gate, ffn_w_up, ffn_w_down, ple_w_proj, ple_w_gate)
    bp = x_prompt.shape[0]
    dtp = x_prompt.dtype
    ssm0 = jnp.zeros((bp, SSD_HEADS, SSD_HEAD_DIM, SSD_STATE), dtp)
    ssd_conv0 = jnp.zeros((bp, SSD_CONV - 1, SSD_CONV_DIM), dtp)
    sc_conv0 = jnp.zeros((bp, SC_CONV - 1, D_MODEL), dtp)
    y_prompt, pk, pv, pssm, pconv, psc = trunk(x_prompt, p_prompt, 0, None, None, ssm0, ssd_conv0, sc_conv0, *weights)
    y_sample, sk, sv, sssm, sconv, ssc = trunk(x_sample, p_sample, PAST_LEN, cache_win_k, cache_win_v,
                                               state_ssm, state_ssd_conv, state_short_conv, *weights)
    return (y_prompt, y_sample, pk, pv, pssm, pconv, psc, sk, sv, sssm, sconv, ssc)
```

```python
import contextlib
import numpy as np
import concourse.bass as bass
import concourse.mybir as mybir
from concourse.bass_utils import run_bass_kernel_spmd

F32 = mybir.dt.float32
BF16 = mybir.dt.bfloat16
AF = mybir.ActivationFunctionType
ALU = mybir.AluOpType

ENGS = ("pe", "act", "dve", "pool", "sp")
NLANES = 8
NCORES = 8
NB = 4
NBP = 4
WITH_SAMPLE = True
DBG_STOP = 0


class StopBuild(Exception):
    pass


def chk(n):
    if DBG_STOP == n:
        raise StopBuild()
SEQ = 2048
DSEQ = 16
PAST = 4096
D = 1024
DFF = 2816
EPS = 1e-6
NSLOT = 5
AHEAD = 2
SLOTW = 4096


class Res:
    __slots__ = ("name", "excl", "w", "r", "rd")

    def __init__(self, name, excl=False):
        self.name = name
        self.excl = excl
        self.w = None
        self.r = {}
        self.rd = []


class Op:
    __slots__ = ("eng", "fn", "deps", "inc", "sem", "val", "dma", "idx", "selfwait")

    def __init__(self, eng, fn, dma):
        self.eng = eng
        self.fn = fn
        self.dma = dma
        self.deps = []
        self.inc = dma
        self.sem = None
        self.val = 0
        self.selfwait = None


class Prog:
    def __init__(self, nc):
        self.nc = nc
        self.ops = []
        self.stack = contextlib.ExitStack()

    def sb(self, name, shape, dtype):
        return self.stack.enter_context(self.nc.sbuf_tensor(name, list(shape), dtype))

    def ps(self, name, shape, dtype=F32):
        return self.stack.enter_context(self.nc.psum_tensor(name, list(shape), dtype))

    def op(self, eng, fn, reads=(), writes=(), dma=False):
        o = Op(eng, fn, dma)
        o.idx = len(self.ops)
        deps = {}

        def add(d, raw):
            if d is None:
                return
            if not d.dma and not dma and d.eng == eng:
                if eng == "pe":
                    return
            deps[d.idx] = d

        for r in reads:
            add(r.w, True)
            if r.excl:
                for e, x in r.r.items():
                    if e != eng:
                        add(x, False)
        for w in writes:
            add(w.w, False)
            for e, x in w.r.items():
                add(x, False)
            for x in w.rd:
                add(x, False)
        best = {}
        for d in deps.values():
            key = ("dma", d.idx) if d.dma else d.eng
            if key not in best or best[key].idx < d.idx:
                best[key] = d
        o.deps = list(best.values())
        for d in o.deps:
            d.inc = True
        for r in reads:
            if dma:
                r.rd.append(o)
            else:
                r.r[eng] = o
        for w in writes:
            w.w = o
            w.r = {}
            w.rd = []
        self.ops.append(o)
        return o

    def dma(self, eng, out, in_, reads=(), writes=(), **kw):
        return self.op(eng, lambda e: e.dma_start(out=out, in_=in_, **kw), reads, writes, dma=True)

    def emit(self):
        nc = self.nc
        st = self.stack
        sems = {e: st.enter_context(nc.semaphore("s_" + e)) for e in ENGS}
        lanes = {e: [st.enter_context(nc.semaphore("l_%s%d" % (e, i))) for i in range(NLANES)] for e in ("sp", "pool")}
        cnt = {e: 0 for e in ENGS}
        lane_use = {e: [0] * NLANES for e in lanes}
        lane_next = {e: 0 for e in lanes}
        for o in self.ops:
            if o.dma:
                l = lane_next[o.eng]
                lane_next[o.eng] = (l + 1) % NLANES
                lane_use[o.eng][l] += 1
                o.sem = lanes[o.eng][l]
                o.val = 16 * lane_use[o.eng][l]
                if lane_use[o.eng][l] > 1:
                    o.selfwait = (o.sem, o.val - 16)
            elif o.inc:
                cnt[o.eng] += 1
                o.sem = sems[o.eng]
                o.val = cnt[o.eng]
        per = {e: [o for o in self.ops if o.eng == e] for e in ENGS}
        finals = []
        for e in lanes:
            for l in range(NLANES):
                if lane_use[e][l]:
                    finals.append((lanes[e][l], 16 * lane_use[e][l]))
        for e in ENGS:
            if cnt[e]:
                finals.append((sems[e], cnt[e]))

        def run(ename, eh):
            waited = {}
            for o in per[ename]:
                ws = [(d.sem, d.val) for d in o.deps]
                if o.selfwait:
                    ws.append(o.selfwait)
                for s, v in ws:
                    k = id(s)
                    if waited.get(k, 0) >= v:
                        continue
                    eh.wait_ge(s, v)
                    waited[k] = v
                ins = o.fn(eh)
                if o.inc:
                    ins.then_inc(o.sem, 16 if o.dma else 1)
            if ename == "sp":
                for s, v in finals:
                    if waited.get(id(s), 0) < v:
                        eh.wait_ge(s, v)

        with nc.Block() as block:
            @block.sync
            def _(e):
                run("sp", e)

            @block.tensor
            def _(e):
                run("pe", e)

            @block.scalar
            def _(e):
                run("act", e)

            @block.vector
            def _(e):
                run("dve", e)

            @block.gpsimd
            def _(e):
                run("pool", e)
        st.close()


def _plan():
    blocks = []
    ar = np.arange

    def qcols(j, rot):
        out = []
        for h in (j, 4 + j):
            d = ar(64)
            if rot:
                d = (d + 32) % 64
            out.append(h * 64 + d)
        return np.concatenate(out)

    def kcols(rot):
        out = []
        for h in range(2):
            d = ar(64)
            if rot:
                d = (d + 32) % 64
            out.append(512 + h * 64 + d)
        return np.concatenate(out)

    r1024 = ar(1024)
    blocks.append(("VD", "hyb_w_in", None, r1024, np.concatenate([640 + ar(128), 3328 + ar(16)])))
    for i in range(3):
        blocks.append(("X%d" % i, "hyb_w_in", None, r1024, 1792 + i * 512 + ar(512)))
    blocks.append(("A0", "hyb_w_in", None, r1024, np.concatenate([qcols(0, 0), qcols(0, 1), qcols(1, 0), qcols(1, 1)])))
    blocks.append(("A1", "hyb_w_in", None, r1024, np.concatenate([qcols(2, 0), qcols(2, 1), qcols(3, 0), qcols(3, 1)])))
    blocks.append(("A2", "hyb_w_in", None, r1024, np.concatenate([kcols(0), kcols(1)])))
    for i in range(2):
        blocks.append(("Z%d" % i, "hyb_w_in", None, r1024, 768 + i * 512 + ar(512)))
    orow = np.concatenate([np.concatenate([j * 64 + ar(64), (4 + j) * 64 + ar(64)]) for j in range(4)] + [512 + ar(1024)])
    for cb in range(2):
        blocks.append(("O%da" % cb, "hyb_w_out", None, orow[0:1024], cb * 512 + ar(512)))
        blocks.append(("O%db" % cb, "hyb_w_out", None, orow[1024:1536], cb * 512 + ar(512)))

    def ffn(l):
        for i in range(11):
            cg = [(2 * i) * 128 + ar(128), (2 * i + 1) * 128 + ar(128)]
            blocks.append(("F%d_%d" % (l, i), ("ffn_w_gate", "ffn_w_up"), l, r1024, np.concatenate([cg[0], cg[0], cg[1], cg[1]])))
        for cb in range(2):
            for gi, (r0, r1) in enumerate(((0, 1024), (1024, 2048), (2048, 2816))):
                blocks.append(("D%d_%d%s" % (l, cb, "abc"[gi]), "ffn_w_down", l, ar(r0, r1), cb * 512 + ar(512)))
        for cb in range(2):
            blocks.append(("G%d_%d" % (l, cb), "ple_w_gate", l, r1024, cb * 512 + ar(512)))
            blocks.append(("P%d_%d" % (l, cb), "ple_w_proj", l, ar(256), cb * 512 + ar(512)))

    ffn(0)
    for c in range(8):
        blocks.append(("S%d" % c, "sc_w_in", None, r1024, np.concatenate([c * 128 + ar(128), 1024 + c * 128 + ar(128), 2048 + c * 128 + ar(128)])))
    for cb in range(2):
        blocks.append(("SO%d" % cb, "sc_w_out", None, r1024, cb * 512 + ar(512)))
    ffn(1)
    return blocks


PLAN = _plan()
BOFF = {}
_o = 0
for _b in PLAN:
    BOFF[_b[0]] = (_o, (len(_b[3]) // 128), len(_b[4]))
    _o += (len(_b[3]) // 128) * len(_b[4])
WTOT = _o


def _build_wsrc(inp):
    w = np.zeros((128, WTOT), np.float32)
    for name, key, l, rows, cols in PLAN:
        off, nkc, ncols = BOFF[name]
        if isinstance(key, tuple):
            g = inp[key[0]][l]
            u = inp[key[1]][l]
            blk = np.empty((len(rows), ncols), np.float32)
            for q in range(4):
                src = g if q % 2 == 0 else u
                blk[:, q * 128:(q + 1) * 128] = src[np.ix_(rows, cols[q * 128:(q + 1) * 128])]
        else:
            src = inp[key] if l is None else inp[key][l]
            blk = src[np.ix_(rows, cols)]
        w[:, off:off + nkc * ncols] = blk.reshape(nkc, 128, ncols).transpose(1, 0, 2).reshape(128, nkc * ncols)
    return w


def build_program():
    nc = bass.Bass("TRN2", target_bir_lowering=False)
    P = Prog(nc)

    def din(name, shape, dt=F32):
        return nc.dram_tensor(name, list(shape), dt, kind="ExternalInput").ap()

    def dout(name, shape):
        return nc.dram_tensor(name, list(shape), F32, kind="ExternalOutput").ap()

    xp = din("xp", [NBP, SEQ, D])
    xs_ = din("xs", [NB, DSEQ, D])
    pp = din("pp", [2, NBP, SEQ, 256])
    psm = din("psm", [2, NB, DSEQ, 256])
    ck = din("ck", [NB, 128, 128])
    cv = din("cv", [NB, 128, 128])
    sssm = din("sssm", [NB, 1024, 128])
    sconv = din("sconv", [NB, 3, 1536])
    ssc = din("ssc", [NB, 2, 1024])
    wsrc = din("wsrc", [128, WTOT])
    ropep = din("ropep", [2, 128, SEQ])
    ropes = din("ropes", [2, 128, 64])
    c_ident = din("c_ident", [128, 128])
    c_mle = din("c_mle", [64, 64])
    c_lst = din("c_lst", [64, 64])
    c_nwfm = din("c_nwfm", [128, 6, 8])
    c_fnw = din("c_fnw", [128, 1024])
    c_snw = din("c_snw", [128, 8])
    c_cw = din("c_cw", [128, 12, 4])
    c_cb = din("c_cb", [128, 12])
    c_scw = din("c_scw", [128, 8, 3])
    c_dtb = din("c_dtb", [64, 16])
    c_alog = din("c_alog", [64, 16])
    c_dsk = din("c_dsk", [64, 16])
    c_sink = din("c_sink", [128, 8])
    wsc = nc.dram_tensor("wsc", [128, WTOT], BF16, kind="Internal").ap()

    yp = dout("yp", [NBP, SEQ, D])
    ys = dout("ys", [NB, DSEQ, D])
    o_pk = dout("o_pk", [NBP, 128, 128])
    o_pv = dout("o_pv", [NBP, 128, 128])
    o_pssm = dout("o_pssm", [NBP, 1024, 128])
    o_pconv = dout("o_pconv", [NBP, 3, 1536])
    o_psc = dout("o_psc", [NBP, 2, 1024])
    o_sk = dout("o_sk", [NB, DSEQ, 128])
    o_sv = dout("o_sv", [NB, DSEQ, 128])
    o_sssm = dout("o_sssm", [NB, 1024, 128])
    o_sconv = dout("o_sconv", [NB, 3, 1536])
    o_ssc = dout("o_ssc", [NB, 2, 1024])

    def T(name, shape, dt=F32):
        return P.sb(name, shape, dt), Res(name)

    wslot = [T("wslot%d" % i, [128, SLOTW], BF16) for i in range(NSLOT)]
    xtok = [T("xtok%d" % i, [128, 4, D]) for i in range(1)]
    ptok, R_ptok = T("ptok", [128, 4, 256])
    xns = [T("xn%d" % i, [128, D], BF16) for i in range(2)]
    statn, R_statn = T("statn", [128, 12])
    statf, R_statf = T("statf", [128, 12])
    xnT, _ = T("xnT", [128, 8, 512], BF16)
    R_xnTs = [Res("xnT%d" % i) for i in range(4)]
    R_xts = [Res("xt%d" % i) for i in range(4)]
    stat, R_stat = T("stat", [128, 16])
    fnw, R_fnw = T("fnw", [128, 1024])
    snw, R_snw = T("snw", [128, 8])
    rope, R_rope = T("rope", [128, 2, 512])
    qT, R_qT = T("qT", [128, 4, 512], BF16)
    kTh, R_kTh = T("kTh", [128, 128 + 512], BF16)
    kTf, R_kTf = T("kTf", [128, 512])
    vh, R_vh = T("vh", [64, 10, 128], BF16)
    vf, R_vf = T("vf", [64, 4, 128])
    gtmp = [T("gtmp%d" % i, [128, 520]) for i in range(5)]
    fm8, R_fm8 = T("fm8", [128, 8, 512], BF16)
    BCt, R_BCt = T("BCt", [128, 4, 512], BF16)
    Btok, R_Btok = T("Btok", [64, 8, 2, 128], BF16)
    xdts = [T("xdt%d" % i, [64, 1024], BF16) for i in range(2)]
    xDs = [T("xD%d" % i, [64, 1024]) for i in range(2)]
    szs = [T("sz%d" % i, [128, 1024]) for i in range(2)]
    Xhi, R_Xb = T("Xhi", [64, 16, 64], BF16)
    Xlo, _ = T("Xlo", [64, 16, 64], BF16)
    dhi, R_dhl = T("dhi", [64, 8, 16], BF16)
    dlo, _ = T("dlo", [64, 8, 16], BF16)
    Wbs = [T("Wb%d" % i, [64, 16, 64], BF16) for i in range(2)]
    cbm, R_cbm = T("cbm", [64, 2, 64], BF16)
    t1, R_t1 = T("t1", [64, 1024])
    t2, R_t2 = T("t2", [64, 1024])
    yn, R_yn = T("yn", [64, 1024], BF16)
    xws = [T("xw%d" % i, [64, 1024], BF16) for i in range(2)]
    wst, R_wst = T("wst", [64, 8, 16])
    HT, R_HT = T("HT", [128, 1024])
    HTb, R_HTb = T("HTb", [128, 1024], BF16)
    dtall, R_dtall = T("dtall", [64, 8, 16])
    dta, R_dta = T("dta", [64, 8, 16])
    ecum, R_ecum = T("ecum", [64, 8, 16])
    erev, R_erev = T("erev", [64, 8, 16])
    decb, R_decb = T("decb", [128, 8, 16])
    pTt = [T("pT%d" % i, [64, 512], BF16) for i in range(3)]
    hT, R_hT = T("hT", [128, 22, 512], BF16)
    aT, R_aT = hT, R_hT
    pT_, R_pT = T("pTp", [128, 2, 512], BF16)
    pb16, R_pb16 = T("pb16", [128, 256], BF16)
    yout = [T("yout%d" % i, [128, D]) for i in range(1)]
    cconv, R_cconv = T("cconv", [128, 12, 4, 3])
    csc, R_csc = T("csc", [128, 8, 4, 2])
    identf, R_c = T("identf", [128, 128])
    identb, _ = T("identb", [128, 128], BF16)
    mle, _ = T("mle", [64, 64])
    lst, _ = T("lst", [64, 64])
    mleb, _ = T("mleb", [64, 64], BF16)
    lstb, _ = T("lstb", [64, 64], BF16)
    ones_f, _ = T("ones_f", [64, 128])
    ones_b, _ = T("ones_b", [64, 128], BF16)
    nwfm, _ = T("nwfm", [128, 6, 8])
    cw, _ = T("cw", [128, 12, 4])
    cbias, _ = T("cbias", [128, 12])
    scw, _ = T("scw", [128, 8, 3])
    dtb, _ = T("dtb", [64, 16])
    abc, _ = T("abc", [64, 16])
    dsk, _ = T("dsk", [64, 16])
    esink, _ = T("esink", [128, 8])
    esb, _ = T("esb", [128, 8, 64])

    hT_f = hT[:].rearrange("p a b -> p (a b)").bitcast(F32)

    stage, R_stage = yout[0][0][:].rearrange('p (c n) -> p c n', c=8), yout[0][1]
    sz_odd = [(yout[0][0][0:64, :], yout[0][1]), (ptok[0:64, :, :].rearrange('p a b -> p (a b)'), R_ptok)]
    pst = P.ps("psall", [128, 8, 512])
    banks = [(pst[:, i, :], Res("bank%d" % i, True)) for i in range(8)]
    bank_i = [0]

    def nb():
        b = banks[bank_i[0] % 8]
        bank_i[0] += 1
        return b

    gt_i = [0]

    def ngt():
        g = gtmp[gt_i[0] % 5]
        gt_i[0] += 1
        return g

    def mm(out, lhsT, rhs, start, stop, R, W):
        P.op("pe", lambda e: e.matmul(out, lhsT=lhsT, rhs=rhs, start=start, stop=stop), R, W)

    def tr(out, in_, ident, R, W):
        P.op("pe", lambda e: e.transpose(out=out, in_=in_, identity=ident), R, W)

    def act(out, in_, func, R, W, bias=None, scale=None, accum=None):
        kw = {}
        if bias is not None:
            kw["bias"] = bias
        if scale is not None:
            kw["scale"] = scale
        if accum is not None:
            kw["accum_out"] = accum
        P.op("act", lambda e: e.activation(out=out, in_=in_, func=func, **kw), R, W)

    def tt(out, in0, in1, op, R, W, eng="dve"):
        P.op(eng, lambda e: e.tensor_tensor(out=out, in0=in0, in1=in1, op=op), R, W)

    def ts(out, in0, s1, op0, R, W, s2=None, op1=None, eng="dve"):
        if op1 is None:
            P.op(eng, lambda e: e.tensor_scalar(out=out, in0=in0, scalar1=s1, scalar2=None, op0=op0), R, W)
        else:
            P.op(eng, lambda e: e.tensor_scalar(out=out, in0=in0, scalar1=s1, scalar2=s2, op0=op0, op1=op1), R, W)

    def stt(out, in0, scalar, in1, op0, op1, R, W, eng="dve"):
        P.op(eng, lambda e: e.scalar_tensor_tensor(out=out, in0=in0, scalar=scalar, in1=in1, op0=op0, op1=op1), R, W)

    def cp(out, in_, R, W, eng="dve"):
        if eng == "act":
            P.op("act", lambda e: e.copy(out=out, in_=in_), R, W)
        else:
            P.op(eng, lambda e: e.tensor_copy(out=out, in_=in_), R, W)

    def memset(ap, val, W, eng="dve"):
        P.op(eng, lambda e: e.memset(ap, val), [], W)

    def dma_rows(dram2d, sbv, nrow, to_sb, R):
        for r_ in range(nrow):
            d = dram2d[r_].rearrange("(c p) -> p c", p=128)
            if to_sb:
                P.dma("sp", sbv[:, :, r_], d, writes=[R], allow_slow_non_contiguous=True)
            else:
                P.dma("pool", d, sbv[:, :, r_], reads=[R], allow_slow_non_contiguous=True)

    for dst, src in ((identf, c_ident), (mle, c_mle), (lst, c_lst), (nwfm, c_nwfm), (cw, c_cw), (cbias, c_cb),
                     (scw, c_scw), (dtb, c_dtb), (abc, c_alog), (dsk, c_dsk), (esink, c_sink)):
        P.dma("sp", dst[:], src, writes=[R_c])
    P.dma("sp", fnw[:], c_fnw, writes=[R_fnw])
    P.dma("sp", snw[:], c_snw, writes=[R_snw])
    cp(identb[:], identf[:], [R_c], [R_c])
    cp(mleb[:], mle[:], [R_c], [R_c])
    cp(lstb[:], lst[:], [R_c], [R_c])
    memset(ones_f[:], 1.0, [R_c])
    memset(ones_b[:], 1.0, [R_c])
    act(abc[:], abc[:], AF.Exp, [R_c], [R_c])
    ts(abc[:], abc[:], -1.0, ALU.mult, [R_c], [R_c])
    act(esink[:], esink[:], AF.Exp, [R_c], [R_c])
    cp(esb[:], esink[:].unsqueeze(2).to_broadcast([128, 8, 64]), [R_c], [R_c])
    RC = [R_c]

    R_blk = {b[0]: Res("blk_" + b[0]) for b in PLAN}
    R_stg = []
    stg_bufs = [(ptok[:].rearrange("p a b -> p (a b)"), R_ptok), (yout[0][0][:, :], yout[0][1])]
    stg_i = [0]

    seq_names = [b[0] for b in PLAN]
    NBLK = len(seq_names)
    wstate = {"loaded": 0, "used": 0, "released": 0}
    ntiles_total = NBP * (SEQ // 512) + (1 if WITH_SAMPLE else 0)
    total_uses = NBLK * ntiles_total

    def wload_upto(n):
        while wstate["loaded"] < min(n, total_uses):
            i = wstate["loaded"]
            assert i - NSLOT < wstate["released"], "weight slot still live"
            name = seq_names[i % NBLK]
            off, nkc, ncols = BOFF[name]
            ln = nkc * ncols
            slot, R_slot = wslot[i % NSLOT]
            if i < NBLK:
                P.dma("pool", slot[:, 0:ln], wsrc[:, off:off + ln], writes=[R_slot])
                P.dma("sp", wsc[:, off:off + ln], slot[:, 0:ln], reads=[R_slot], writes=[R_blk[name]])
            else:
                P.dma("sp", slot[:, 0:ln], wsc[:, off:off + ln], reads=[R_blk[name]], writes=[R_slot])
            wstate["loaded"] += 1

    def wget(name):
        i = wstate["used"]
        assert seq_names[i % NBLK] == name, (seq_names[i % NBLK], name)
        wload_upto(i + 1 + AHEAD)
        wstate["used"] += 1
        off, nkc, ncols = BOFF[name]
        slot, R_slot = wslot[i % NSLOT]
        return slot[:, 0:nkc * ncols].rearrange("p (k c) -> p k c", k=nkc), R_slot

    def wdone(n=1):
        wstate["released"] += n

    class Tile:
        pass

    def RXN(c0, n_):
        return R_xnTs[c0 // 128:(c0 + n_ - 1) // 128 + 1]

    class NormPipe:
        def __init__(self, tl, xt, widx):
            self.tl, self.xt, self.widx = tl, xt, widx
            self.pend = []

        def sub(self, n):
            tl, xt = self.tl, self.xt
            c0, ts_ = tl.subs[n]
            j_, Rj = ngt()
            jb = j_[:].bitcast(BF16)
            act(jb[0:ts_, 0:D], xt[0:ts_, n, :], AF.Square, [R_xts[n]], [Rj, R_statn], accum=statn[0:ts_, n:n + 1])
            act(statn[0:ts_, 4 + n:5 + n], statn[0:ts_, n:n + 1], AF.Ln, [R_statn], [R_statn], scale=1.0 / D, bias=EPS)
            act(statn[0:ts_, 8 + n:9 + n], statn[0:ts_, 4 + n:5 + n], AF.Exp, [R_statn], [R_statn], scale=-0.5)
            xnb, R_xnb = xns[n % 2]
            if n % 2 == 0:
                ts(xnb[0:ts_, :], xt[0:ts_, n, :], statn[0:ts_, 8 + n:9 + n], ALU.mult, [R_xts[n], R_statn], [R_xnb])
            else:
                act(xnb[0:ts_, :], xt[0:ts_, n, :], AF.Identity, [R_xts[n], R_statn], [R_xnb], scale=statn[0:ts_, 8 + n:9 + n])
            self.pend.append(n)

        def flush_one(self):
            if not self.pend:
                return
            n = self.pend.pop(0)
            c0, ts_ = self.tl.subs[n]
            xnb, R_xnb = xns[n % 2]
            bk, Rb = nb()
            pb = bk.bitcast(BF16)
            for kc in range(8):
                tr(pb[:, kc * 128:kc * 128 + ts_], xnb[0:ts_, kc * 128:(kc + 1) * 128], identb[0:ts_, 0:ts_], [R_xnb] + RC, [Rb])
            tt(xnT[:, :, c0:c0 + ts_], pb.rearrange("p (k t) -> p k t", k=8)[:, :, 0:ts_],
               nwfm[:, self.widx, :].unsqueeze(2).to_broadcast([128, 8, ts_]), ALU.mult, [Rb] + RC, [R_xnTs[n]])

        def step(self, n):
            self.flush_one()
            self.sub(n)

        def finish(self):
            while self.pend:
                self.flush_one()

    def rmsnorm_T(tl, xt, widx):
        npipe = NormPipe(tl, xt, widx)
        for n in range(len(tl.subs)):
            npipe.step(n)
        npipe.finish()

    def proj_fm(tl, blk, R_blk_, nkc, c, actT, R_actT):
        bk, Rb = nb()
        for kc in range(nkc):
            mm(bk[:, 0:tl.T], blk[:, kc, c * 128:(c + 1) * 128], actT[:, kc, 0:tl.T], kc == 0, kc == nkc - 1, [R_blk_] + R_actT, [Rb])
        return bk, Rb

    def resid_tm(tl, xt, blks, actT, R_actT, kc_lists, cb, after=None):
        for n, (c0, ts_) in enumerate(tl.subs):
            bk, Rb = nb()
            tot = sum(len(k) for k in kc_lists)
            i = 0
            for (blk, R_b), kcs in zip(blks, kc_lists):
                for j, kc in enumerate(kcs):
                    mm(bk[0:ts_, :], actT[:, kc, c0:c0 + ts_], blk[:, j, :], i == 0, i == tot - 1, [R_b, R_actT], [Rb])
                    i += 1
            xa = xt[0:ts_, n, cb * 512:(cb + 1) * 512]
            tt(xa, xa, bk[0:ts_, :], ALU.add, [Rb, R_xts[n]], [R_xts[n]])
            if after is not None:
                after(n)

    def hybrid(tl, xt):
        T_ = tl.T
        rmsnorm_T(tl, xt, 0)
        chk(1)
        P.dma("sp", rope[:, :, 0:T_], tl.rope_src, writes=[R_rope])
        blk, Rw = wget("VD")
        nch = len(tl.chunks)
        vbanks = {}
        bdt, Rbdt = nb()
        for ci, (seg, c0, cl) in enumerate(tl.chunks):
            if ci % 4 == 0:
                vbanks[ci // 4] = nb()
            bv, Rbv = vbanks[ci // 4]
            for kc in range(8):
                mm(bv[0:cl, (ci % 4) * 128:(ci % 4 + 1) * 128], xnT[:, kc, c0:c0 + cl], blk[:, kc, 0:128], kc == 0, kc == 7, [Rw] + RXN(c0, cl), [Rbv])
            for kc in range(8):
                mm(bdt[0:cl, ci * 16:(ci + 1) * 16], xnT[:, kc, c0:c0 + cl], blk[:, kc, 128:144], kc == 0, kc == 7, [Rw] + RXN(c0, cl), [Rbdt])
        wdone()
        cl = tl.chunks[0][2]
        for q in vbanks:
            bv, Rbv = vbanks[q]
            n4 = min(4, nch - 4 * q)
            cp(vh[0:cl, 2 + 4 * q:2 + 4 * q + n4, :], bv[0:cl, 0:n4 * 128].rearrange("p (c d) -> p c d", c=n4), [Rbv], [R_vh], eng="act")
        if tl.sample:
            bv, Rbv = vbanks[0]
            cp(vf[0:16, 0:4, :], bv[0:16, 0:512].rearrange("p (c d) -> p c d", c=4), [Rbv], [R_vf], eng="act")
        elif tl.last:
            bv, Rbv = vbanks[1]
            cp(vf[:, 0:2, :], bv[0:64, 256:512].rearrange("p (c d) -> p c d", c=2), [Rbv], [R_vf], eng="act")
        dsl = (slice(0, cl), slice(0, nch), slice(None))
        tt(dtall[dsl], bdt[0:cl, 0:nch * 16].rearrange("p (c h) -> p c h", c=nch), dtb[0:cl, :].unsqueeze(1).to_broadcast([cl, nch, 16]),
           ALU.add, [Rbdt] + RC, [R_dtall])
        act(dtall[dsl], dtall[dsl], AF.Exp, [R_dtall], [R_dtall])
        act(dtall[dsl], dtall[dsl], AF.Ln, [R_dtall], [R_dtall], bias=1.0)
        tt(dta[dsl], dtall[dsl], abc[0:cl, :].unsqueeze(1).to_broadcast([cl, nch, 16]), ALU.mult, [R_dtall] + RC, [R_dta])
        cp(dhi[dsl], dta[dsl], [R_dta], [R_dhl])
        tt(dlo[dsl], dta[dsl], dhi[dsl], ALU.subtract, [R_dta, R_dhl], [R_dhl])
        chk(3)
        bs, Rbs = nb()
        dflat = dta[0:cl, 0:nch, :].rearrange("p c h -> p (c h)")
        W_ = nch * 16
        mm(bs[0:cl, 0:W_], mle[0:cl, 0:cl], dflat, True, True, [R_dta] + RC, [Rbs])
        mm(bs[0:cl, 128:128 + W_], lst[0:cl, 0:cl], dflat, True, True, [R_dta] + RC, [Rbs])
        mm(bs[:, 256:256 + W_], ones_f[0:cl, :], dflat, True, True, [R_dta] + RC, [Rbs])
        act(ecum[dsl], bs[0:cl, 0:W_].rearrange("p (c h) -> p c h", c=nch), AF.Exp, [Rbs], [R_ecum])
        act(erev[dsl], bs[0:cl, 128:128 + W_].rearrange("p (c h) -> p c h", c=nch), AF.Exp, [Rbs], [R_erev])
        act(decb[:, 0:nch, :], bs[:, 256:256 + W_].rearrange("p (c h) -> p c h", c=nch), AF.Exp, [Rbs], [R_decb])
        tt(wst[dsl], dtall[dsl], erev[dsl], ALU.mult, [R_dtall, R_erev], [R_wst])
        chk(4)
        nseg, L = tl.nseg, tl.L
        for bi_ in range(3):
            blk, Rw = wget("X%d" % bi_)
            for cc in range(4):
                c = bi_ * 4 + cc
                bk, Rb = proj_fm(tl, blk, Rw, 8, cc, xnT, R_xnTs[0:len(tl.subs)])
                (g, Rg), (a, Ra) = ngt(), ngt()
                gv = g[:, 0:nseg * (3 + L)].rearrange("p (s l) -> p s l", s=nseg)
                av = a[:, 0:T_].rearrange("p (s l) -> p s l", s=nseg)
                cp(gv[:, :, 0:3], cconv[:, c, 0:nseg, :], [R_cconv], [Rg], eng="pool")
                cp(gv[:, :, 3:3 + L], bk[:, 0:T_].rearrange("p (s l) -> p s l", s=nseg), [Rb], [Rg], eng="act")
                cp(cconv[:, c, 0:nseg, :], gv[:, :, L:L + 3], [Rg], [R_cconv], eng="pool")
                act(av, gv[:, :, 0:L], AF.Identity, [Rg] + RC, [Ra], scale=cw[:, c, 0:1])
                stt(av, gv[:, :, 1:1 + L], cw[:, c, 1:2], av, ALU.mult, ALU.add, [Rg, Ra] + RC, [Ra])
                stt(av, gv[:, :, 2:2 + L], cw[:, c, 2:3], av, ALU.mult, ALU.add, [Rg, Ra] + RC, [Ra])
                stt(av, gv[:, :, 3:3 + L], cw[:, c, 3:4], av, ALU.mult, ALU.add, [Rg, Ra] + RC, [Ra])
                dst, Rd = (fm8[:, c, 0:T_], R_fm8) if c < 8 else (BCt[:, c - 8, 0:T_], R_BCt)
                act(dst, a[:, 0:T_], AF.Silu, [Ra] + RC, [Rd], bias=cbias[:, c:c + 1])
            wdone()
        chk(5)
        for bname, pairs in (("A0", (0, 1)), ("A1", (2, 3)), ("A2", (4,))):
            blk, Rw = wget(bname)
            for pi, j in enumerate(pairs):
                b1, Rb1 = proj_fm(tl, blk, Rw, 8, 2 * pi, xnT, R_xnTs[0:len(tl.subs)])
                b2, Rb2 = proj_fm(tl, blk, Rw, 8, 2 * pi + 1, xnT, R_xnTs[0:len(tl.subs)])
                (g1, Rg1), (g2, Rg2) = ngt(), ngt()
                tt(g1[:, 0:T_], b1[:, 0:T_], rope[:, 0, 0:T_], ALU.mult, [Rb1, R_rope], [Rg1])
                tt(g2[:, 0:T_], b2[:, 0:T_], rope[:, 1, 0:T_], ALU.mult, [Rb2, R_rope], [Rg2])
                if j < 4:
                    tt(qT[:, j, 0:T_], g1[:, 0:T_], g2[:, 0:T_], ALU.add, [Rg1, Rg2], [R_qT], eng="pool")
                else:
                    tt(kTf[:, 0:T_], g1[:, 0:T_], g2[:, 0:T_], ALU.add, [Rg1, Rg2], [R_kTf], eng="pool")
                    cp(kTh[:, 128:128 + T_], kTf[:, 0:T_], [R_kTf], [R_kTh], eng="act")
            wdone()
        chk(2)
        for ci, (seg, c0, cl) in enumerate(tl.chunks):
            if ci % 4 == 0:
                bk, Rb = nb()
                pbv = bk.bitcast(BF16)
            for g_ in range(2):
                tr(pbv[0:cl, ((ci % 4) * 2 + g_) * 128:((ci % 4) * 2 + g_ + 1) * 128], BCt[:, g_, c0:c0 + cl], identb[:, :], [R_BCt] + RC, [Rb])
            if ci % 4 == 3 or ci == nch - 1:
                n4 = ci % 4 + 1
                cp(Btok[0:cl, ci - n4 + 1:ci + 1, :, :], pbv[0:cl, 0:n4 * 256].rearrange("p (c g n) -> p c g n", c=n4, g=2), [Rb], [R_Btok])
        z0, Rz0 = wget("Z0")
        chk(9)
        z1, Rz1 = wget("Z1")
        zb = ((z0, Rz0), (z1, Rz1))
        chs = tl.chunks
        nchk = len(chs)
        if tl.sample:
            for ci in range(nchk):
                sample_seg_begin(tl, chs[ci][0])
                ssd_A(tl, ci, *chs[ci], zb)
                attn_B1(tl, ci, *chs[ci])
                attn_B2(tl, ci, *chs[ci])
                ssd_C1(tl, ci, *chs[ci])
                ssd_C2(tl, ci, *chs[ci])
                sample_seg_end(tl, chs[ci][0], chs[ci][1])
        else:
            ssd_A(tl, 0, *chs[0], zb)
            attn_B1(tl, 0, *chs[0])
            ssd_A(tl, 1, *chs[1], zb)
            attn_B2(tl, 0, *chs[0])
            for ci in range(nchk):
                ssd_C1(tl, ci, *chs[ci])
                if ci + 1 < nchk:
                    attn_B1(tl, ci + 1, *chs[ci + 1])
                if ci + 2 < nchk:
                    ssd_A(tl, ci + 2, *chs[ci + 2], zb)
                if ci + 1 < nchk:
                    attn_B2(tl, ci + 1, *chs[ci + 1])
                ssd_C2(tl, ci, *chs[ci])
        wdone(2)
        if not tl.sample:
            cp(kTh[:, 0:128], kTh[:, T_:T_ + 128], [R_kTh], [R_kTh], eng="act")
            cp(vh[:, 0:2, :], vh[:, 8:10, :], [R_vh], [R_vh], eng="act")
            if tl.last:
                prompt_seq_end(tl)
        for cb in range(2):
            ba = wget("O%da" % cb)
            bb = wget("O%db" % cb)
            npipe = NormPipe(tl, xt, 1) if cb == 1 else None
            resid_tm(tl, xt, (ba, bb), aT, R_aT, (list(range(8)), list(range(8, 12))), cb, after=(npipe.step if npipe else None))
            if npipe:
                npipe.finish()
            wdone(2)
        chk(8)

    att_ctx = {}

    def sz_of(tl, ci):
        if tl.sample:
            return szs[ci % 2]
        n = ci // 2
        if ci % 2 == 0:
            return szs[n % 2]
        return sz_odd[n % 2]


    def attn_B1(tl, ci, seg, c0, cl):
        if tl.sample:
            kbs = [(0, 64, vh[0:64, 0, :]), (64, 64, vh[0:64, 1, :]), (128 + c0, cl, vh[0:cl, 2 + ci, :])]
        else:
            kbs = []
            for back in (2, 1, 0):
                if tl.chunk0 + ci - back >= 0:
                    kbs.append((128 + c0 - back * 64, 64, vh[0:64, 2 + ci - back, :]))
        NQ = 4 * cl
        pts = []
        for bi_, (k0, nk, vap) in enumerate(kbs):
            pt, Rp = pTt[bi_]
            for kv in range(2):
                bk, Rb = nb()
                mm(bk[0:nk, 0:NQ].rearrange("p (j q) -> p j q", j=4), kTh[kv * 64:(kv + 1) * 64, k0:k0 + nk],
                   qT[kv * 64:(kv + 1) * 64, :, c0:c0 + cl], True, True, [R_kTh, R_qT], [Rb])
                act(pt[0:nk, kv * NQ:(kv + 1) * NQ], bk[0:nk, 0:NQ], AF.Exp, [Rb], [Rp], scale=0.125)
            pts.append((pt, Rp, nk, vap))
        att_ctx[ci] = pts

    def attn_B2(tl, ci, seg, c0, cl):
        pts = att_ctx.pop(ci)
        NQ = 4 * cl
        bo, Rbo = nb()
        bd, Rbd = nb()
        for i, (pt, Rp, nk, vap) in enumerate(pts):
            mm(bo[:, 0:2 * NQ], vap, pt[0:nk, 0:2 * NQ], i == 0, i == len(pts) - 1, [Rp, R_vh], [Rbo])
        for i, (pt, Rp, nk, vap) in enumerate(pts):
            mm(bd[:, 0:2 * NQ], ones_b[0:nk, :], pt[0:nk, 0:2 * NQ], i == 0, i == len(pts) - 1, [Rp] + RC, [Rbd])
        den, R_den = ngt()
        for kv in range(2):
            ps_ = slice(kv * 64, (kv + 1) * 64)
            tt(den[ps_, 0:NQ].rearrange("p (h q) -> p h q", h=4), bd[ps_, kv * NQ:(kv + 1) * NQ].rearrange("p (h q) -> p h q", h=4),
               esb[ps_, kv * 4:(kv + 1) * 4, 0:cl], ALU.add, [Rbd] + RC, [R_den])
        act(den[:, 0:NQ], den[:, 0:NQ], AF.Ln, [R_den], [R_den])
        act(den[:, 0:NQ], den[:, 0:NQ], AF.Exp, [R_den], [R_den], scale=-1.0)
        for kv in range(2):
            ps_ = slice(kv * 64, (kv + 1) * 64)
            tt(aT[ps_, 0:4, c0:c0 + cl], bo[ps_, kv * NQ:(kv + 1) * NQ].rearrange("p (j q) -> p j q", j=4),
               den[ps_, 0:NQ].rearrange("p (j q) -> p j q", j=4), ALU.mult, [Rbo, R_den], [R_aT] + R_stg)

    def ssd_A(tl, ci, seg, c0, cl, zb):
        (xdt, R_xdt), (xD, R_xD), (Wb, R_Wb), (xw, R_xw) = xdts[ci % 2], xDs[ci % 2], Wbs[ci % 2], xws[ci % 2]
        bk, Rb = nb()
        pbv = bk.bitcast(BF16)
        for fc in range(8):
            tr(pbv[0:cl, fc * 128:(fc + 1) * 128], fm8[:, fc, c0:c0 + cl], identb[:, :], [R_fm8] + RC, [Rb])
        x3 = pbv[0:cl, :].rearrange("p (h d) -> p h d", h=16)
        tt(xdt[0:cl, :].rearrange("p (h d) -> p h d", h=16), x3, dtall[0:cl, ci, :].unsqueeze(2).to_broadcast([cl, 16, 64]), ALU.mult,
           [Rb, R_dtall], [R_xdt])
        tt(xD[0:cl, :].rearrange("p (h d) -> p h d", h=16), x3, dsk[0:cl, :].unsqueeze(2).to_broadcast([cl, 16, 64]), ALU.mult,
           [Rb] + RC, [R_xD])
        tt(xw[0:cl, :].rearrange("p (h d) -> p h d", h=16), x3, wst[0:cl, ci, :].unsqueeze(2).to_broadcast([cl, 16, 64]), ALU.mult,
           [Rb, R_wst], [R_xw])
        tt(Xhi[0:cl, :, 0:cl], dhi[0:cl, ci, :].unsqueeze(2).to_broadcast([cl, 16, cl]), mleb[0:cl, 0:cl].unsqueeze(1).to_broadcast([cl, 16, cl]),
           ALU.mult, [R_dhl] + RC, [R_Xb], eng="pool")
        tt(Xlo[0:cl, :, 0:cl], dlo[0:cl, ci, :].unsqueeze(2).to_broadcast([cl, 16, cl]), mleb[0:cl, 0:cl].unsqueeze(1).to_broadcast([cl, 16, cl]),
           ALU.mult, [R_dhl] + RC, [R_Xb], eng="pool")
        bc_, Rbc = nb()
        for g_ in range(2):
            mm(bc_[0:cl, g_ * 64:g_ * 64 + cl], BCt[:, g_, c0:c0 + cl], BCt[:, 2 + g_, c0:c0 + cl], True, True, [R_BCt], [Rbc])
        tt(cbm[0:cl, :, 0:cl], bc_[0:cl, 0:128].rearrange("p (g t) -> p g t", g=2)[:, :, 0:cl],
           mle[0:cl, 0:cl].unsqueeze(1).to_broadcast([cl, 2, cl]), ALU.mult, [Rbc] + RC, [R_cbm])
        hs_per = 512 // cl if cl == 64 else 16
        nbk = 16 // hs_per
        for q in range(nbk):
            bs, Rbs = nb()
            h0, h1 = q * hs_per, (q + 1) * hs_per
            mm(bs[0:cl, 0:hs_per * cl].rearrange("p (h t) -> p h t", h=hs_per), lstb[0:cl, 0:cl], Xhi[0:cl, h0:h1, 0:cl], True, False, [R_Xb] + RC, [Rbs])
            mm(bs[0:cl, 0:hs_per * cl].rearrange("p (h t) -> p h t", h=hs_per), lstb[0:cl, 0:cl], Xlo[0:cl, h0:h1, 0:cl], False, True, [R_Xb] + RC, [Rbs])
            act(Wb[0:cl, h0:h1, 0:cl], bs[0:cl, 0:hs_per * cl].rearrange("p (h t) -> p h t", h=hs_per), AF.Exp, [Rbs], [R_Wb])
        for g_ in range(2):
            tt(Wb[0:cl, g_ * 8:(g_ + 1) * 8, 0:cl], Wb[0:cl, g_ * 8:(g_ + 1) * 8, 0:cl], cbm[0:cl, g_, 0:cl].unsqueeze(1).to_broadcast([cl, 8, cl]),
               ALU.mult, [R_Wb, R_cbm], [R_Wb])
        if tl.sample:
            sz, R_sz = szs[ci % 2]
            for half in range(2):
                zw, Rzw = zb[half]
                bz, Rbz = nb()
                for kc in range(8):
                    mm(bz[0:cl, :], xnT[:, kc, c0:c0 + cl], zw[:, kc, :], kc == 0, kc == 7, [Rzw] + RXN(c0, cl), [Rbz])
                act(sz[0:cl, half * 512:(half + 1) * 512], bz[0:cl, :], AF.Silu, [Rbz], [R_sz])
        elif ci % 2 == 0:
            n = ci // 2
            szp, R_szp = szs[n % 2]
            for half in range(2):
                zw, Rzw = zb[half]
                bz, Rbz = nb()
                for kc in range(8):
                    mm(bz[:, :], xnT[:, kc, c0:c0 + 128], zw[:, kc, :], kc == 0, kc == 7, [Rzw, R_xnTs[n]], [Rbz])
                act(szp[:, half * 512:(half + 1) * 512], bz[:, :], AF.Silu, [Rbz], [R_szp])
            so, R_so = sz_odd[n % 2]
            P.dma("sp", so, szp[64:128, :], reads=[R_szp], writes=[R_so])

    def ssd_C1(tl, ci, seg, c0, cl):
        (xdt, R_xdt), (xD, R_xD), (Wb, R_Wb), (xw, R_xw) = xdts[ci % 2], xDs[ci % 2], Wbs[ci % 2], xws[ci % 2]
        ybanks = []
        for g_ in range(2):
            by, Rby = nb()
            for hh in range(8):
                h = g_ * 8 + hh
                mm(by[0:cl, hh * 64:(hh + 1) * 64], Wb[0:cl, h, 0:cl], xdt[0:cl, h * 64:(h + 1) * 64], True, True, [R_Wb, R_xdt], [Rby])
            bi2, Rbi = nb()
            mm(bi2[0:cl, :], BCt[:, 2 + g_, c0:c0 + cl], HTb[:, g_ * 512:(g_ + 1) * 512], True, True, [R_BCt, R_HTb], [Rbi])
            ybanks.append((by, Rby, bi2, Rbi))
        tt(HT[:, :].rearrange("p (h d) -> p h d", h=16), HT[:, :].rearrange("p (h d) -> p h d", h=16),
           decb[:, ci, :].unsqueeze(2).to_broadcast([128, 16, 64]), ALU.mult, [R_HT, R_decb], [R_HT], eng="pool")
        sbanks = []
        for g_ in range(2):
            bS, RbS = nb()
            mm(bS[:, :], Btok[0:cl, ci, g_, :], xw[0:cl, g_ * 512:(g_ + 1) * 512], True, True, [R_Btok, R_xw], [RbS])
            sbanks.append((bS, RbS))
        for g_ in range(2):
            by, Rby, bi2, Rbi = ybanks[g_]
            sl = slice(g_ * 512, (g_ + 1) * 512)
            tt(t1[0:cl, sl], by[0:cl, :], xD[0:cl, sl], ALU.add, [Rby, R_xD], [R_t1])
            tt(t2[0:cl, sl].rearrange("p (h d) -> p h d", h=8), bi2[0:cl, :].rearrange("p (h d) -> p h d", h=8),
               ecum[0:cl, ci, g_ * 8:(g_ + 1) * 8].unsqueeze(2).to_broadcast([cl, 8, 64]), ALU.mult, [Rbi, R_ecum], [R_t2])
        tt(t1[0:cl, :], t1[0:cl, :], t2[0:cl, :], ALU.add, [R_t1, R_t2], [R_t1])
        sz, R_sz = sz_of(tl, ci)
        tt(t2[0:cl, :], t1[0:cl, :], sz[0:cl, :], ALU.mult, [R_t1, R_sz], [R_t2])
        for g_ in range(2):
            sl = slice(g_ * 512, (g_ + 1) * 512)
            act(t1[0:cl, sl], t2[0:cl, sl], AF.Square, [R_t2], [R_t1, R_stat], accum=stat[0:cl, 4 + g_:5 + g_])
        act(stat[0:cl, 6:8], stat[0:cl, 4:6], AF.Ln, [R_stat], [R_stat], scale=1.0 / 512, bias=EPS)
        act(stat[0:cl, 8:10], stat[0:cl, 6:8], AF.Exp, [R_stat], [R_stat], scale=-0.5)
        for g_ in range(2):
            bS, RbS = sbanks[g_]
            sl = slice(g_ * 512, (g_ + 1) * 512)
            tt(HT[:, sl], HT[:, sl], bS[:, :], ALU.add, [RbS, R_HT], [R_HT])
        cp(HTb[:, :], HT[:, :], [R_HT], [R_HTb], eng="act")
        for g_ in range(2):
            sl = slice(g_ * 512, (g_ + 1) * 512)
            ts(yn[0:cl, sl], t2[0:cl, sl], stat[0:cl, 8 + g_:9 + g_], ALU.mult, [R_t2, R_stat], [R_yn])

    def ssd_C2(tl, ci, seg, c0, cl):
        bk, Rb = nb()
        pbv = bk.bitcast(BF16)
        for fc in range(8):
            tr(pbv[:, fc * 64:fc * 64 + cl], yn[0:cl, fc * 128:(fc + 1) * 128], identb[0:cl, 0:cl], [R_yn] + RC, [Rb])
        tt(aT[:, 4:12, c0:c0 + cl], pbv[:, 0:512].rearrange("p (f t) -> p f t", f=8)[:, :, 0:cl],
           snw[:, :].unsqueeze(2).to_broadcast([128, 8, cl]), ALU.mult, [Rb, R_snw], [R_aT] + R_stg)

    def prompt_seq_begin():
        memset(cconv[:], 0.0, [R_cconv])
        memset(csc[:], 0.0, [R_csc])
        memset(HT[:], 0.0, [R_HT])
        memset(HTb[:], 0.0, [R_HTb])

    def store_state(dst):
        for half in range(2):
            bk, Rb = nb()
            for q in range(4):
                fc = half * 4 + q
                tr(bk[:, q * 128:(q + 1) * 128], HT[:, fc * 128:(fc + 1) * 128], identf[:, :], [R_HT] + RC, [Rb])
            cp(stage[:, half * 4:half * 4 + 4, :], bk[:, :].rearrange("p (c n) -> p c n", c=4), [Rb], [R_stage])
        P.dma("pool", dst.rearrange("(c q) n -> q c n", q=128), stage[:], reads=[R_stage])

    def prompt_seq_end(tl):
        b = tl.b
        bk, Rb = nb()
        tr(bk[:, 0:128], kTf[:, 384:512], identf[:, :], [R_kTf] + RC, [Rb])
        g, Rg = ngt()
        cp(g[:, 0:128], bk[:, 0:128], [Rb], [Rg])
        P.dma("pool", o_pk[b], g[:, 0:128], reads=[Rg])
        P.dma("pool", o_pv[b].rearrange("(c p) d -> p c d", p=64), vf[:, 0:2, :], reads=[R_vf])
        store_state(o_pssm[b])
        dma_rows(o_pconv[b], cconv[:, :, 0, :], 3, False, R_cconv)

    def sample_seg_begin(tl, seg):
        b = seg
        g, Rg = ngt()
        P.dma("sp", g[:, 0:128], ck[b], writes=[Rg])
        bk, Rb = nb()
        tr(bk[:, 0:128], g[:, 0:128], identf[:, :], [Rg] + RC, [Rb])
        cp(kTh[:, 0:128], bk[:, 0:128], [Rb], [R_kTh], eng="act")
        g2, Rg2 = ngt()
        P.dma("sp", g2[0:64, 0:256].rearrange("p (c d) -> p c d", c=2), cv[b].rearrange("(c p) d -> p c d", p=64), writes=[Rg2])
        cp(vh[:, 0:2, :], g2[0:64, 0:256].rearrange("p (c d) -> p c d", c=2), [Rg2], [R_vh])
        P.dma("sp", stage[:], sssm[b].rearrange("(c q) n -> q c n", q=128), writes=[R_stage])
        for half in range(2):
            bk, Rb = nb()
            for q in range(4):
                tr(bk[:, q * 128:(q + 1) * 128], stage[:, half * 4 + q, :], identf[:, :], [R_stage] + RC, [Rb])
            cp(HT[:, half * 512:(half + 1) * 512], bk[:, :], [Rb], [R_HT])
        cp(HTb[:, :], HT[:, :], [R_HT], [R_HTb], eng="act")

    def sample_seg_end(tl, seg, c0):
        b = seg
        bk, Rb = nb()
        tr(bk[0:16, 0:128], kTf[:, c0:c0 + 16], identf[:, :], [R_kTf] + RC, [Rb])
        g, Rg = ngt()
        cp(g[0:16, 0:128], bk[0:16, 0:128], [Rb], [Rg])
        P.dma("pool", o_sk[b], g[0:16, 0:128], reads=[Rg])
        P.dma("pool", o_sv[b], vf[0:16, seg, :], reads=[R_vf])
        store_state(o_sssm[b])

    def short_conv(tl, xt):
        T_, nseg, L = tl.T, tl.nseg, tl.L
        for c in range(8):
            blk, Rw = wget("S%d" % c)
            bb_, Rbb = proj_fm(tl, blk, Rw, 8, 0, xnT, R_xnTs[0:len(tl.subs)])
            bc_, Rbc = proj_fm(tl, blk, Rw, 8, 1, xnT, R_xnTs[0:len(tl.subs)])
            bh_, Rbh = proj_fm(tl, blk, Rw, 8, 2, xnT, R_xnTs[0:len(tl.subs)])
            wdone()
            (hs, Rhs), (g, Rg), (a, Ra) = ngt(), ngt(), ngt()
            cp(hs[:, 0:T_], bh_[:, 0:T_], [Rbh], [Rhs], eng="act")
            gv = g[:, 0:nseg * (2 + L)].rearrange("p (s l) -> p s l", s=nseg)
            av = a[:, 0:T_].rearrange("p (s l) -> p s l", s=nseg)
            cp(gv[:, :, 0:2], csc[:, c, 0:nseg, :], [R_csc], [Rg], eng="pool")
            tt(gv[:, :, 2:2 + L], bc_[:, 0:T_].rearrange("p (s l) -> p s l", s=nseg), hs[:, 0:T_].rearrange("p (s l) -> p s l", s=nseg),
               ALU.mult, [Rbc, Rhs], [Rg])
            cp(csc[:, c, 0:nseg, :], gv[:, :, L:L + 2], [Rg], [R_csc], eng="pool")
            act(av, gv[:, :, 0:L], AF.Identity, [Rg] + RC, [Ra], scale=scw[:, c, 0:1])
            stt(av, gv[:, :, 1:1 + L], scw[:, c, 1:2], av, ALU.mult, ALU.add, [Rg, Ra] + RC, [Ra])
            stt(av, gv[:, :, 2:2 + L], scw[:, c, 2:3], av, ALU.mult, ALU.add, [Rg, Ra] + RC, [Ra])
            tt(fm8[:, c, 0:T_], bb_[:, 0:T_], a[:, 0:T_], ALU.mult, [Rbb, Ra], [R_fm8])
        if tl.last:
            if tl.sample:
                for s in range(nseg):
                    dma_rows(o_ssc[s], csc[:, :, s, :], 2, False, R_csc)
            else:
                dma_rows(o_psc[tl.b], csc[:, :, 0, :], 2, False, R_csc)
        for cb in range(2):
            bw = wget("SO%d" % cb)
            npipe = NormPipe(tl, xt, 4) if cb == 1 else None
            resid_tm(tl, xt, (bw,), fm8, R_fm8, (list(range(8)),), cb, after=(npipe.step if npipe else None))
            if npipe:
                npipe.finish()
            wdone()

    def ffn_ple(tl, xt, l, ple_after):
        T_ = tl.T
        for i in range(11):
            blk, Rw = wget("F%d_%d" % (l, i))
            for q in range(2):
                c = 2 * i + q
                bg, Rbg = proj_fm(tl, blk, Rw, 8, 2 * q, xnT, R_xnTs[0:len(tl.subs)])
                bu, Rbu = proj_fm(tl, blk, Rw, 8, 2 * q + 1, xnT, R_xnTs[0:len(tl.subs)])
                sg, Rsg = ngt()
                act(sg[:, 0:T_], bg[:, 0:T_], AF.Silu, [Rbg], [Rsg])
                tt(hT[:, c, 0:T_], sg[:, 0:T_], bu[:, 0:T_], ALU.mult, [Rsg, Rbu], [R_hT] + R_stg)
            wdone()
        for cb in range(2):
            bl = [wget("D%d_%d%s" % (l, cb, s)) for s in "abc"]
            npipe = NormPipe(tl, xt, 2 + 3 * l) if cb == 1 else None
            resid_tm(tl, xt, bl, hT, R_hT, (list(range(8)), list(range(8, 16)), list(range(16, 22))), cb, after=(npipe.step if npipe else None))
            if npipe:
                npipe.finish()
            wdone(3)
        P.dma("sp", ptok[0:tl.subs[0][1], 0:len(tl.subs), :], tl.p_src[l], writes=[R_ptok])
        for n, (c0, ts_) in enumerate(tl.subs):
            cp(pb16[0:ts_, :], ptok[0:ts_, n, :], [R_ptok], [R_pb16])
            bk, Rb = nb()
            pbv = bk.bitcast(BF16)
            for kc in range(2):
                tr(pbv[:, kc * 128:kc * 128 + ts_], pb16[0:ts_, kc * 128:(kc + 1) * 128], identb[0:ts_, 0:ts_], [R_pb16] + RC, [Rb])
            cp(pT_[:, :, c0:c0 + ts_], pbv[:, 0:256].rearrange("p (k t) -> p k t", k=2)[:, :, 0:ts_], [Rb], [R_pT], eng="act")
        for cb in range(2):
            gw, Rgw = wget("G%d_%d" % (l, cb))
            pw, Rpw = wget("P%d_%d" % (l, cb))
            pend = None
            for n, (c0, ts_) in enumerate(tl.subs):
                bg, Rbg = nb()
                for kc in range(8):
                    mm(bg[0:ts_, :], xnT[:, kc, c0:c0 + ts_], gw[:, kc, :], kc == 0, kc == 7, [Rgw, R_xnTs[n]], [Rbg])
                bp, Rbp = nb()
                for kc in range(2):
                    mm(bp[0:ts_, :], pT_[:, kc, c0:c0 + ts_], pw[:, kc, :], kc == 0, kc == 1, [Rpw, R_pT], [Rbp])
                sg, Rsg = ngt()
                act(sg[0:ts_, 0:512], bg[0:ts_, :], AF.Exp, [Rbg], [Rsg], scale=-1.0)
                act(sg[0:ts_, 0:512], sg[0:ts_, 0:512], AF.Ln, [Rsg], [Rsg], bias=1.0)
                act(sg[0:ts_, 0:512], sg[0:ts_, 0:512], AF.Exp, [Rsg], [Rsg], scale=-1.0)
                tt(sg[0:ts_, 0:512], sg[0:ts_, 0:512], bp[0:ts_, :], ALU.mult, [Rsg, Rbp], [Rsg])
                xa = xt[0:ts_, n, cb * 512:(cb + 1) * 512]
                tt(xa, xa, sg[0:ts_, 0:512], ALU.add, [Rsg, R_xts[n]], [R_xts[n]])
                if cb == 1:
                    if pend is not None:
                        ple_after.step(pend)
                    pend = n
            if cb == 1 and pend is not None:
                ple_after.step(pend)
            wdone(2)
        ple_after.finish()

    class FinalPipe:
        def __init__(self, tl, xt, nxt):
            self.tl, self.xt, self.nxt = tl, xt, nxt
            self.obufs = [(yout[0][0][:, :], yout[0][1]), (ptok[:].rearrange("p a b -> p (a b)"), R_ptok), (rope[:].rearrange("p a b -> p (a b)"), R_rope)]

        def step(self, n):
            tl, xt = self.tl, self.xt
            c0, ts_ = tl.subs[n]
            j_, Rj = ngt()
            jb = j_[:].bitcast(BF16)
            act(jb[0:ts_, 0:D], xt[0:ts_, n, :], AF.Square, [R_xts[n]], [Rj, R_statf], accum=statf[0:ts_, n:n + 1])
            act(statf[0:ts_, 4 + n:5 + n], statf[0:ts_, n:n + 1], AF.Ln, [R_statf], [R_statf], scale=1.0 / D, bias=EPS)
            act(statf[0:ts_, 8 + n:9 + n], statf[0:ts_, 4 + n:5 + n], AF.Exp, [R_statf], [R_statf], scale=-0.5)
            yo, Ryo = self.obufs[n % 3]
            stt(yo[0:ts_, :], xt[0:ts_, n, :], statf[0:ts_, 8 + n:9 + n], fnw[0:ts_, :], ALU.mult, ALU.mult, [R_xts[n], R_statf, R_fnw], [Ryo])
            P.dma("pool", tl.y_dst(n), yo[0:ts_, :], reads=[Ryo])
            if self.nxt is not None and n < len(self.nxt.subs):
                load_x(self.nxt, xt, n)

        def finish(self):
            if self.nxt is not None:
                for n in range(len(self.tl.subs), len(self.nxt.subs)):
                    load_x(self.nxt, self.xt, n)

    def load_x(tl, xt, n):
        c0, ts_ = tl.subs[n]
        P.dma("sp", xt[0:ts_, n, :], tl.x_rows(n), writes=[R_xts[n]])

    tiles = []
    for b in range(NBP):
        for ti in range(SEQ // 512):
            tl = Tile()
            tl.sample = False
            tl.b = b
            tl.T = 512
            tl.nseg = 1
            tl.L = 512
            tl.chunk0 = ti * 8
            tl.first = ti == 0
            tl.last = ti == SEQ // 512 - 1
            tl.subs = [(n * 128, 128) for n in range(4)]
            tl.chunks = [(0, c * 64, 64) for c in range(8)]
            t0 = ti * 512
            tl.x_rows = (lambda n, b=b, t0=t0: xp[b, t0 + n * 128:t0 + (n + 1) * 128, :])
            tl.p_src = [pp[l, b, t0:t0 + 512, :].rearrange("(n p) d -> p n d", p=128) for l in range(2)]
            tl.rope_src = ropep[:, :, t0:t0 + 512].rearrange("a p t -> p a t")
            tl.y_dst = (lambda n, b=b, t0=t0: yp[b, t0 + n * 128:t0 + (n + 1) * 128, :])
            tiles.append(tl)
    tl = Tile()
    tl.sample = True
    tl.b = None
    tl.T = 64
    tl.nseg = 4
    tl.L = 16
    tl.chunk0 = 0
    tl.first = True
    tl.last = True
    tl.subs = [(0, 64)]
    tl.chunks = [(s, s * 16, 16) for s in range(4)]
    tl.x_rows = (lambda n: xs_.rearrange("b t d -> (b t) d"))
    tl.p_src = [psm[l].rearrange("b t d -> (b t) d").unsqueeze(1) for l in range(2)]
    tl.rope_src = ropes.rearrange("a p t -> p a t")
    tl.y_dst = (lambda n: ys.rearrange("b t d -> (b t) d"))
    if WITH_SAMPLE:
        tiles.append(tl)

    try:
        xt = xtok[0][0]
        for n in range(len(tiles[0].subs)):
            load_x(tiles[0], xt, n)
        for i, tl in enumerate(tiles):
            nxt = tiles[i + 1] if i + 1 < len(tiles) else None
            if tl.sample:
                for s_ in range(4):
                    dma_rows(sconv[s_], cconv[:, :, s_, :], 3, True, R_cconv)
                    dma_rows(ssc[s_], csc[:, :, s_, :], 2, True, R_csc)
            elif tl.first:
                prompt_seq_begin()
            hybrid(tl, xt)
            if tl.sample:
                for s in range(4):
                    dma_rows(o_sconv[s], cconv[:, :, s, :], 3, False, R_cconv)
            ffn_ple(tl, xt, 0, NormPipe(tl, xt, 3))
            short_conv(tl, xt)
            ffn_ple(tl, xt, 1, FinalPipe(tl, xt, nxt))
        assert wstate["used"] == total_uses, (wstate, total_uses)
    except StopBuild:
        pass
    P.emit()
    return nc


_CACHE = {}


def _consts(inp):
    c = {}
    c["c_ident"] = np.eye(128, dtype=np.float32)
    k = np.arange(64)
    c["c_mle"] = (k[:, None] <= k[None, :]).astype(np.float32)
    c["c_lst"] = (k[:, None] > k[None, :]).astype(np.float32)
    half = 32
    inv = (np.float32(10000.0) ** (-np.arange(half, dtype=np.float32) / np.float32(half))).astype(np.float32)

    def table(pos):
        ang = pos.astype(np.float32)[None, :] * inv[:, None]
        cos = np.cos(ang).astype(np.float32)
        sin = np.sin(ang).astype(np.float32)
        d = np.arange(128) % 64
        f = d % 32
        sign = np.where(d < 32, -1.0, 1.0).astype(np.float32)
        return np.stack([cos[f], sin[f] * sign[:, None]]).astype(np.float32)

    c["ropep"] = table(np.arange(SEQ))
    c["ropes"] = np.tile(table(PAST + np.arange(DSEQ)), (1, 1, 4))
    nw = np.stack([inp["norm_mix"][0], inp["norm_ffn"][0], inp["norm_ple"][0], inp["norm_mix"][1], inp["norm_ffn"][1], inp["norm_ple"][1]])
    c["c_nwfm"] = np.ascontiguousarray(nw.reshape(6, 8, 128).transpose(2, 0, 1))
    c["c_fnw"] = np.ascontiguousarray(np.broadcast_to(inp["final_norm"][None, :], (128, 1024)))
    c["c_snw"] = np.ascontiguousarray(inp["ssd_norm_w"].reshape(8, 128).T)
    c["c_cw"] = np.ascontiguousarray(inp["ssd_conv_w"].reshape(4, 12, 128).transpose(2, 1, 0))
    c["c_cb"] = np.ascontiguousarray(inp["ssd_conv_b"].reshape(12, 128).T)
    c["c_scw"] = np.ascontiguousarray(inp["sc_conv_w"].reshape(3, 8, 128).transpose(2, 1, 0))
    c["c_dtb"] = np.ascontiguousarray(np.broadcast_to(inp["ssd_dt_bias"][None, :], (64, 16)))
    c["c_alog"] = np.ascontiguousarray(np.broadcast_to(inp["ssd_a_log"][None, :], (64, 16)))
    c["c_dsk"] = np.ascontiguousarray(np.broadcast_to(inp["ssd_d"][None, :], (64, 16)))
    sk = inp["attn_sinks"]
    c["c_sink"] = np.ascontiguousarray(np.broadcast_to(sk[None, :], (128, 8)))
    return {k_: np.ascontiguousarray(v, dtype=np.float32) for k_, v in c.items()}


def kernel(**inp):
    inp = {k: np.asarray(v) for k, v in inp.items()}
    if "nc" not in _CACHE:
        _CACHE["nc"] = build_program()
    nc = _CACHE["nc"]
    shared = _consts(inp)
    shared["wsrc"] = _build_wsrc(inp)
    in_maps = []
    for c in range(NCORES):
        s = slice(c * NB, (c + 1) * NB)
        m = dict(shared)
        m["xp"] = np.ascontiguousarray(inp["x_prompt"][s])
        m["xs"] = np.ascontiguousarray(inp["x_sample"][s])
        m["pp"] = np.ascontiguousarray(inp["p_prompt"][:, s])
        m["psm"] = np.ascontiguousarray(inp["p_sample"][:, s])
        m["ck"] = np.ascontiguousarray(inp["cache_win_k"][s]).reshape(NB, 128, 128)
        m["cv"] = np.ascontiguousarray(inp["cache_win_v"][s]).reshape(NB, 128, 128)
        m["sssm"] = np.ascontiguousarray(inp["state_ssm"][s]).reshape(NB, 1024, 128)
        m["sconv"] = np.ascontiguousarray(inp["state_ssd_conv"][s])
        m["ssc"] = np.ascontiguousarray(inp["state_short_conv"][s])
        in_maps.append(m)
    res = run_bass_kernel_spmd(nc, in_maps, core_ids=list(range(NCORES)))
    R = res.results

    def cat(key, shape):
        return np.concatenate([np.asarray(r[key], dtype=np.float32) for r in R], axis=0).reshape(shape)

    B = NB * NCORES
    return (cat("yp", (B, SEQ, D)), cat("ys", (B, DSEQ, D)),
            cat("o_pk", (B, 128, 2, 64)), cat("o_pv", (B, 128, 2, 64)),
            cat("o_pssm", (B, 16, 64, 128)), cat("o_pconv", (B, 3, 1536)), cat("o_psc", (B, 2, 1024)),
            cat("o_sk", (B, DSEQ, 2, 64)), cat("o_sv", (B, DSEQ, 2, 64)),
            cat("o_sssm", (B, 16, 64, 128)), cat("o_sconv", (B, 3, 1536)), cat("o_ssc", (B, 2, 1024)))
```
